# Optimizing a Trainium2 kernel written in Bass

```python
import math
import jax, jax.numpy as jnp
from jax import lax
import numpy as np


D_MODEL = 2048
BATCH = 4
SEQ = 2048
DEPTH = 1
DEC_BATCH = 16
DEC_SEQ = 64
PAST_LEN = 2048

CHUNK = 64
N_META = 16
MIX_WIDTH = D_MODEL
POOL_WIDTH = MIX_WIDTH // 2
POOL_WINDOWS = (2, 4, 8, 16)
POOL_GROUPS = len(POOL_WINDOWS)
POOL_GROUP_DIM = POOL_WIDTH // POOL_GROUPS
POOL_STATE = max(POOL_WINDOWS) - 1
ATTN_WIDTH = MIX_WIDTH - POOL_WIDTH
DIFF_HEADS = 8
V_DIM = ATTN_WIDTH // DIFF_HEADS
HALF_DIM = V_DIM // 2
QK_DIM = 2 * HALF_DIM
QK_WIDTH = DIFF_HEADS * QK_DIM
IN_WIDTH = POOL_WIDTH + 2 * QK_WIDTH + ATTN_WIDTH
ATT_SCALE = HALF_DIM ** -0.5
D_FF = -(-8 * D_MODEL // (3 * 256)) * 256
NUM_BUCKETS = 32
MAX_DISTANCE = 128
QB = 128
EPS = 1e-6
SUBLN_EPS = 1e-5
NEG_INF = -1e30

kernel_name = "hymba_pool_diffattn_stream_step"


def _rmsnorm(x, g, eps=EPS):
    xf = x.astype(jnp.float32)
    y = xf * lax.rsqrt(jnp.mean(xf * xf, axis=-1, keepdims=True) + eps)
    return (y * g.astype(jnp.float32)).astype(x.dtype)


def _chunk_id(idx):
    return jnp.where(idx < N_META, -1, (idx - N_META) // CHUNK)


def _t5_bucket(rel):
    half = NUM_BUCKETS // 2
    max_exact = half // 2
    ret = jnp.where(rel > 0, half, 0)
    n = jnp.abs(rel)
    nf = jnp.maximum(n, 1).astype(jnp.float32)
    large = max_exact + (jnp.log(nf / max_exact) / math.log(MAX_DISTANCE / max_exact)
                         * (half - max_exact)).astype(jnp.int32)
    large = jnp.minimum(large, half - 1)
    return ret + jnp.where(n < max_exact, n, large)


def _rel_bias(rel, table):
    return jnp.transpose(table[_t5_bucket(rel)], (2, 0, 1))


def _project(h, g_mix, w_in):
    b, t, _ = h.shape
    proj = _rmsnorm(h, g_mix) @ w_in
    u = proj[..., :POOL_WIDTH]
    q = proj[..., POOL_WIDTH:POOL_WIDTH + QK_WIDTH].reshape(b, t, DIFF_HEADS, 2, HALF_DIM)
    k = proj[..., POOL_WIDTH + QK_WIDTH:POOL_WIDTH + 2 * QK_WIDTH].reshape(b, t, DIFF_HEADS, QK_DIM)
    v = proj[..., POOL_WIDTH + 2 * QK_WIDTH:].reshape(b, t, DIFF_HEADS, V_DIM)
    return u, q, k, v


def _pool_mixer(u, w_pool, pool_scale):
    b, L, _ = u.shape
    uf = u.astype(jnp.float32)
    cs = jnp.concatenate([jnp.zeros((b, 1, POOL_WIDTH), jnp.float32), jnp.cumsum(uf, axis=1)], axis=1)
    idx = jnp.arange(L)
    means = []
    for g, w in enumerate(POOL_WINDOWS):
        csg = cs[..., g * POOL_GROUP_DIM:(g + 1) * POOL_GROUP_DIM]
        lo = jnp.maximum(idx + 1 - w, 0)
        cnt = (idx + 1 - lo).astype(jnp.float32)
        means.append((csg[:, 1:] - csg[:, lo]) / cnt[None, :, None])
    p = (jnp.concatenate(means, axis=-1) - uf).astype(u.dtype)
    p = p.reshape(b, L, POOL_GROUPS, POOL_GROUP_DIM)
    y = jnp.einsum('blgc,gcd->blgd', p, w_pool).reshape(b, L, POOL_WIDTH)
    return y * pool_scale


def _diff_attention(q, k, v, bias, mask, lam, g_subln, lambda_init):
    b, tq = q.shape[:2]
    lk = k.shape[1]
    k2 = k.reshape(b, lk, DIFF_HEADS, 2, HALF_DIM)
    logits = (jnp.einsum('bqhmd,bkhmd->bhmqk', q, k2).astype(jnp.float32) * ATT_SCALE
              + bias[None, :, None].astype(jnp.float32))
    logits = jnp.where(mask, logits, NEG_INF)
    a = jax.nn.softmax(logits, axis=-1)
    wts = a[:, :, 0] - lam * a[:, :, 1]
    o = jnp.einsum('bhqk,bkhe->bqhe', wts.astype(v.dtype), v)
    o = _rmsnorm(o, g_subln, SUBLN_EPS) * (1.0 - lambda_init)
    return o.reshape(b, tq, ATTN_WIDTH)


def _prompt_attention(q, k, v, rel_table, lam, g_subln, lambda_init):
    b, L = q.shape[:2]
    nb = -(-L // QB)
    Lp = nb * QB
    q_pad = jnp.pad(q, ((0, 0), (0, Lp - L), (0, 0), (0, 0), (0, 0)))
    kidx = jnp.arange(L)
    kchunk = _chunk_id(kidx)

    def block(bi):
        qb = lax.dynamic_slice_in_dim(q_pad, bi * QB, QB, axis=1)
        qidx = bi * QB + jnp.arange(QB)
        mask = kchunk[None, :] <= _chunk_id(qidx)[:, None]
        bias = _rel_bias(kidx[None, :] - qidx[:, None], rel_table)
        return _diff_attention(qb, k, v, bias, mask, lam, g_subln, lambda_init)

    out = lax.map(block, jnp.arange(nb))
    return jnp.transpose(out, (1, 0, 2, 3)).reshape(b, Lp, ATTN_WIDTH)[:, :L]


def _swiglu(h, g_ffn, w_gate, w_up, w_down):
    n = _rmsnorm(h, g_ffn)
    return h + (jax.nn.silu(n @ w_gate) * (n @ w_up)) @ w_down


def setup_inputs(seed: int = 0) -> dict:
    key = jax.random.key(seed)
    ks = jax.random.split(key, 24)
    f32 = jnp.float32
    nrm = lambda k, s, sc: jax.random.normal(k, s, f32) * sc
    gain = lambda k, s: 1.0 + 0.1 * jax.random.normal(k, s, f32)
    return {
        "x_prompt": nrm(ks[0], (BATCH, SEQ, D_MODEL), 1.0),
        "x_sample": nrm(ks[1], (DEC_BATCH, DEC_SEQ, D_MODEL), 1.0),
        "cache_k": nrm(ks[2], (DEPTH, DEC_BATCH, N_META + PAST_LEN, DIFF_HEADS, QK_DIM), 1.0),
        "cache_v": nrm(ks[3], (DEPTH, DEC_BATCH, N_META + PAST_LEN, DIFF_HEADS, V_DIM), 1.0),
        "state_pool": nrm(ks[4], (DEPTH, DEC_BATCH, POOL_STATE, POOL_WIDTH), 1.0),
        "meta": nrm(ks[5], (N_META, D_MODEL), 1.0),
        "g_mix": gain(ks[6], (DEPTH, D_MODEL)),
        "w_in": nrm(ks[7], (DEPTH, D_MODEL, IN_WIDTH), D_MODEL ** -0.5),
        "w_pool": nrm(ks[8], (DEPTH, POOL_GROUPS, POOL_GROUP_DIM, POOL_GROUP_DIM), POOL_GROUP_DIM ** -0.5),
        "pool_scale": gain(ks[9], (DEPTH, POOL_WIDTH)),
        "lambda_q1": nrm(ks[10], (DEPTH, HALF_DIM), 0.1),
        "lambda_k1": nrm(ks[11], (DEPTH, HALF_DIM), 0.1),
        "lambda_q2": nrm(ks[12], (DEPTH, HALF_DIM), 0.1),
        "lambda_k2": nrm(ks[13], (DEPTH, HALF_DIM), 0.1),
        "g_subln": gain(ks[14], (DEPTH, V_DIM)),
        "w_out": nrm(ks[15], (DEPTH, MIX_WIDTH, D_MODEL), MIX_WIDTH ** -0.5),
        "g_ffn": gain(ks[16], (DEPTH, D_MODEL)),
        "w_gate": nrm(ks[17], (DEPTH, D_MODEL, D_FF), D_MODEL ** -0.5),
        "w_up": nrm(ks[18], (DEPTH, D_MODEL, D_FF), D_MODEL ** -0.5),
        "w_down": nrm(ks[19], (DEPTH, D_FF, D_MODEL), D_FF ** -0.5),
        "rel_bias": nrm(ks[20], (NUM_BUCKETS, DIFF_HEADS), 0.5),
        "g_final": gain(ks[21], (D_MODEL,)),
    }


def reference(x_prompt, x_sample, cache_k, cache_v, state_pool, meta, g_mix, w_in, w_pool,
              pool_scale, lambda_q1, lambda_k1, lambda_q2, lambda_k2, g_subln, w_out, g_ffn,
              w_gate, w_up, w_down, rel_bias, g_final):
    f32 = jnp.float32
    b = x_prompt.shape[0]
    h_p = jnp.concatenate(
        [jnp.broadcast_to(meta[None].astype(x_prompt.dtype), (b, N_META, D_MODEL)), x_prompt], axis=1)
    h_s = x_sample
    kp_l, vp_l, pp_l, ks_l, vs_l, ps_l = [], [], [], [], [], []
    for l in range(DEPTH):
        lambda_init = 0.8 - 0.6 * math.exp(-0.3 * l)
        lam = (jnp.exp(jnp.sum(lambda_q1[l].astype(f32) * lambda_k1[l].astype(f32)))
               - jnp.exp(jnp.sum(lambda_q2[l].astype(f32) * lambda_k2[l].astype(f32)))
               + lambda_init)

        u_p, q_p, k_p, v_p = _project(h_p, g_mix[l], w_in[l])
        a_p = _pool_mixer(u_p, w_pool[l], pool_scale[l])
        b_p = _prompt_attention(q_p, k_p, v_p, rel_bias, lam, g_subln[l], lambda_init)
        h_p = h_p + jnp.concatenate([a_p, b_p], axis=-1) @ w_out[l]
        h_p = _swiglu(h_p, g_ffn[l], w_gate[l], w_up[l], w_down[l])

        u_s, q_s, k_s, v_s = _project(h_s, g_mix[l], w_in[l])
        t = h_s.shape[1]
        ctx = jnp.concatenate([state_pool[l].astype(u_s.dtype), u_s], axis=1)
        a_s = _pool_mixer(ctx, w_pool[l], pool_scale[l])[:, POOL_STATE:]
        k_all = jnp.concatenate([cache_k[l].astype(k_s.dtype), k_s], axis=1)
        v_all = jnp.concatenate([cache_v[l].astype(v_s.dtype), v_s], axis=1)
        lk = k_all.shape[1]
        kidx = jnp.arange(lk)
        qidx = lk - t + jnp.arange(t)
        mask = _chunk_id(kidx)[None, :] <= _chunk_id(qidx)[:, None]
        bias = _rel_bias(kidx[None, :] - qidx[:, None], rel_bias)
        b_s = _diff_attention(q_s, k_all, v_all, bias, mask, lam, g_subln[l], lambda_init)
        h_s = h_s + jnp.concatenate([a_s, b_s], axis=-1) @ w_out[l]
        h_s = _swiglu(h_s, g_ffn[l], w_gate[l], w_up[l], w_down[l])

        kp_l.append(k_p)
        vp_l.append(v_p)
        pp_l.append(u_p[:, -POOL_STATE:])
        ks_l.append(k_s)
        vs_l.append(v_s)
        ps_l.append(ctx[:, -POOL_STATE:])

    y_prompt = _rmsnorm(h_p[:, N_META:], g_final)
    y_sample = _rmsnorm(h_s, g_final)
    k_prompt = jnp.stack(kp_l, axis=0)
    v_prompt = jnp.stack(vp_l, axis=0)
    pool_prompt = jnp.stack(pp_l, axis=0)
    k_sample = jnp.stack(ks_l, axis=0)
    v_sample = jnp.stack(vs_l, axis=0)
    pool_sample = jnp.stack(ps_l, axis=0)
    return (y_prompt, y_sample, k_prompt, v_prompt, pool_prompt, k_sample, v_sample, pool_sample)
```

```python
import math
import numpy as np
import concourse.bass as bass
import concourse.mybir as mybir
from concourse.bass_utils import run_bass_kernel_spmd

F32 = mybir.dt.float32
BF16 = mybir.dt.bfloat16
AF = mybir.ActivationFunctionType
ALU = mybir.AluOpType
AX = mybir.AxisListType

D = 2048
NOWN = 1024
C_OWN, C_FOR, C_META, C_HALO, C_SAMP, NTOK = 0, 1024, 2048, 2064, 2080, 2208
NQ = 1152
NKV_OUT = 1168
DFF = 5632
EPS = 1e-6
SUBLN_EPS = 1e-5
ATT_SCALE = 64 ** -0.5
LAMBDA_INIT = 0.8 - 0.6 * math.exp(0.0)
NEG = -1e30
OH_OWN, OH_F7, OH_META, OH_FCOL, LTOT = 0, 1280, 1920, 2560, 2576
UW = 1200


def _t5_bucket_np(rel):
    rel = np.asarray(rel, dtype=np.int64)
    half, max_exact = 16, 8
    ret = np.where(rel > 0, half, 0)
    n = np.abs(rel)
    nf = np.maximum(n, 1).astype(np.float32)
    large = max_exact + (np.log(nf / np.float32(max_exact)) / np.float32(math.log(128 / max_exact))
                         * np.float32(half - max_exact)).astype(np.int32)
    large = np.minimum(large, half - 1)
    return (ret + np.where(n < max_exact, n, large)).astype(np.int64)


def _bucket_table():
    rel = np.arange(-2300, 2301)
    try:
        import jax
        import jax.numpy as jnp
        cpu = jax.devices("cpu")[0]
        with jax.default_device(cpu):
            r = jnp.asarray(rel, dtype=jnp.int32)
            half, max_exact = 16, 8
            ret = jnp.where(r > 0, half, 0)
            n = jnp.abs(r)
            nf = jnp.maximum(n, 1).astype(jnp.float32)
            large = max_exact + (jnp.log(nf / max_exact) / math.log(128 / max_exact)
                                 * (half - max_exact)).astype(jnp.int32)
            large = jnp.minimum(large, half - 1)
            out = np.asarray(ret + jnp.where(n < max_exact, n, large)).astype(np.int64)
        return out
    except Exception:
        return _t5_bucket_np(rel)


_BT = None


def _bucket(rel):
    global _BT
    if _BT is None:
        _BT = _bucket_table()
    return _BT[np.asarray(rel) + 2300]


def _onehot_for_half(half):
    idx = np.zeros(LTOT, dtype=np.int64)
    j = np.arange(1280)
    idx[OH_OWN:OH_OWN + 1280] = _bucket(511 - j)
    j = np.arange(640)
    idx[OH_F7:OH_F7 + 640] = _bucket(-j - 1) if half == 1 else 32
    idx[OH_META:OH_META + 640] = _bucket(-j - 1) if half == 0 else 15
    idx[OH_FCOL:LTOT] = 15 if half == 1 else 32
    oh = np.zeros((33, LTOT), dtype=np.float32)
    oh[idx, np.arange(LTOT)] = 1.0
    return oh


def _flat(toks):
    for t in toks:
        if t is None:
            continue
        if isinstance(t, list):
            yield from _flat(t)
        else:
            yield t


class Eng:
    def __init__(self, raw, sem, automark):
        self.raw, self.sem, self.cnt, self.seen, self.automark = raw, sem, 0, {}, automark

    def wait(self, *toks):
        for sem, v in _flat(toks):
            if self.seen.get(sem, 0) >= v:
                continue
            self.raw.wait_ge(sem, v)
            self.seen[sem] = v

    def mark(self, inst):
        self.cnt += 1
        inst.then_inc(self.sem, 1)
        return (self.sem, self.cnt)

    def __call__(self, inst):
        return self.mark(inst) if self.automark else None

    @property
    def last(self):
        return (self.sem, self.cnt) if self.cnt else None


class DSem:
    def __init__(self, sem):
        self.sem, self.n = sem, 0

    def inc(self, inst):
        self.n += 1
        inst.then_inc(self.sem, 16)
        return (self.sem, 16 * self.n)

    @property
    def total(self):
        return (self.sem, 16 * self.n) if self.n else None


class Bank:
    def __init__(self, t):
        self.t = t
        self.free = []

    @property
    def f32(self):
        return self.t[:]

    @property
    def bf(self):
        return self.t[:].bitcast(BF16)


class Ring:
    def __init__(self, items):
        self.items, self.i = items, 0

    def next(self):
        it = self.items[self.i % len(self.items)]
        self.i += 1
        return it


class Slot:
    def __init__(self, t, dsem):
        self.t, self.ds = t, dsem
        self.free = []
        self.ready = None


def build_program(dbg=None, stop_after=None, nheads=8):
    dbg = dbg or ()
    nc = bass.Bass("TRN2", target_bir_lowering=False)

    def din(name, shape):
        return nc.dram_tensor(name, list(shape), F32, kind="ExternalInput")

    def dout(name, shape):
        return nc.dram_tensor(name, list(shape), F32, kind="ExternalOutput")

    xin = din("xin", [NTOK, D])
    ck = din("ck", [2, 2064, 8, 128])
    cv = din("cv", [2, 2064, 8, 128])
    stp = din("stp", [2, 15, 1024])
    gmix = din("g_mix", [D])
    w_in = din("w_in", [D, 4096])
    w_pool = din("w_pool", [4, 256, 256])
    pool_scale = din("pool_scale", [1024])
    lam_in = din("lam_in", [4, 64])
    gsub = din("g_subln", [128])
    w_out = din("w_out", [D, D])
    gffn = din("g_ffn", [D])
    w_gate = din("w_gate", [D, DFF])
    w_up = din("w_up", [D, DFF])
    w_down = din("w_down", [DFF, D])
    relb = din("rel_bias", [32, 8])
    gfin = din("g_final", [D])
    ohd = din("onehot", [33, LTOT])

    y_o = dout("y", [NQ, D])
    k_o = dout("kout", [NKV_OUT, 8, 128])
    v_o = dout("vout", [NKV_OUT, 8, 128])
    p_o = dout("pout", [48, 1024])
    rscr = nc.dram_tensor("rscr", [8, 128, LTOT], F32, kind="Internal")
    dbg_t = {}

    def dbg_out(name, shape):
        dbg_t[name] = dout("dbg_" + name, shape)
        return dbg_t[name]

    nsem = [0]

    def newsem(name):
        nsem[0] += 1
        return nc.alloc_semaphore(name)

    PE = Eng(nc.tensor, newsem("s_pe"), False)
    ACT = Eng(nc.scalar, newsem("s_act"), True)
    DVE = Eng(nc.vector, newsem("s_dve"), True)
    POOL = Eng(nc.gpsimd, newsem("s_pool"), True)
    SP = Eng(nc.sync, newsem("s_sp"), False)
    ENGS = [PE, ACT, DVE, POOL, SP]

    class _Px:
        def __init__(self, eng):
            self._e = eng

        def __getattr__(self, name):
            f = getattr(self._e.raw, name)
            e = self._e

            def w(*a, **k):
                e.wait(e.last)
                return f(*a, **k)
            return w

    nv, ns, ng = _Px(DVE), _Px(ACT), _Px(POOL)
    DVE.px, ACT.px, POOL.px = nv, ns, ng
    dsems = []

    def newdsem(name):
        d = DSem(newsem(name))
        dsems.append(d)
        return d

    def barrier():
        toks = [e.last for e in ENGS] + [d.total for d in dsems]
        for e in ENGS:
            e.wait(*[t for t in toks if t is not None and t[0] is not e.sem])

    class Mem:
        def __init__(self):
            self.ptr = 0
            self.n = 0

        def alloc(self, name, shape, dt, at=None):
            size = int(np.prod(shape[1:])) * (2 if dt == BF16 else 4)
            if at is None:
                at = (self.ptr + 63) // 64 * 64
                self.ptr = at + size
            assert at + size <= 212800, (name, at, size)
            self.n += 1
            return nc.alloc_sbuf_tensor_at(f"{name}_{self.n}", list(shape), dt, offset=16512 + at)

    mem = Mem()
    K = 1024

    banks = [Bank(nc.alloc_psum_tensor(f"bank{i}", [128, 512], F32)) for i in range(8)]

    ident_f = mem.alloc("ident_f", [128, 128], F32)
    ident_b = mem.alloc("ident_b", [128, 128], BF16)
    ones_b = mem.alloc("ones_b", [128, 128], BF16)
    ones_f = mem.alloc("ones_f", [128, 128], F32)
    cols = mem.alloc("cols", [128, 160], F32)
    C_NEGLAM, C_GCOL, C_EPS, C_SEPS, C_PSC, C_CFAR, C_FB, C_SSQ = 0, 1, 2, 3, 4, 12, 20, 28
    assert mem.ptr <= 3 * K, mem.ptr
    abT = mem.alloc("abT", [128, 16, NQ], BF16, at=3 * K)
    xnT = mem.alloc("xnT", [128, 16, NTOK], BF16, at=39 * K)
    XBASE = 108 * K
    mem.ptr = XBASE

    ds_misc = newdsem("d_misc")

    POOL(ng.memset(ident_f[:], 1.0))
    t_id = POOL(ng.affine_select(out=ident_f[:], in_=ident_f[:], pattern=[[-1, 128]],
                                        compare_op=ALU.is_equal, fill=0.0, base=0, channel_multiplier=1))
    t_o1 = POOL(ng.memset(ones_f[:], 1.0))
    DVE.wait(t_id, t_o1)
    DVE(nv.tensor_copy(out=ident_b[:], in_=ident_f[:]))
    DVE(nv.tensor_copy(out=ones_b[:], in_=ones_f[:]))
    DVE(nv.memset(cols[:, C_EPS:C_EPS + 1], EPS))
    DVE(nv.memset(cols[:, C_SEPS:C_SEPS + 1], 128.0 * SUBLN_EPS))

    lamv = mem.alloc("lamv", [128, 4, 64], F32)
    lamt = mem.alloc("lamt", [128, 8], F32)
    with nc.allow_non_contiguous_dma(reason="tiny param loads"):
        ds_misc.inc(nc.sync.dma_start(out=cols[:, C_PSC:C_PSC + 8],
                                      in_=pool_scale.ap().rearrange("(c p) -> p c", p=128)))
        ds_misc.inc(nc.sync.dma_start(out=cols[:, C_GCOL:C_GCOL + 1],
                                      in_=gsub.ap().rearrange("(p o) -> p o", o=1)))
    ds_misc.inc(nc.sync.dma_start(out=lamv[:].rearrange("p a b -> p (a b)"),
                                  in_=lam_in.ap().rearrange("a b -> (a b)").partition_broadcast(128)))
    t_par = ds_misc.total
    DVE.wait(t_par)
    DVE(nv.tensor_tensor(out=lamv[:, 0, :], in0=lamv[:, 0, :], in1=lamv[:, 1, :], op=ALU.mult))
    t1 = DVE(nv.tensor_tensor(out=lamv[:, 2, :], in0=lamv[:, 2, :], in1=lamv[:, 3, :], op=ALU.mult))
    DVE.wait(t1)
    DVE(nv.reduce_sum(out=lamt[:, 0:1], in_=lamv[:, 0, :], axis=AX.X))
    t1 = DVE(nv.reduce_sum(out=lamt[:, 1:2], in_=lamv[:, 2, :], axis=AX.X))
    ACT.wait(t1)
    t1 = ACT(ns.activation(out=lamt[:, 2:4], in_=lamt[:, 0:2], func=AF.Exp))
    DVE.wait(t1)
    t1 = DVE(nv.tensor_tensor(out=lamt[:, 4:5], in0=lamt[:, 3:4], in1=lamt[:, 2:3], op=ALU.subtract))
    DVE.wait(t1)
    DVE(nv.tensor_scalar(out=cols[:, C_NEGLAM:C_NEGLAM + 1], in0=lamt[:, 4:5],
                                scalar1=-LAMBDA_INIT, scalar2=None, op0=ALU.add))
    DVE(nv.tensor_scalar(out=cols[:, C_GCOL:C_GCOL + 1], in0=cols[:, C_GCOL:C_GCOL + 1],
                                scalar1=(1.0 - LAMBDA_INIT) * math.sqrt(128.0), scalar2=None, op0=ALU.mult))

    p0_mark = mem.ptr
    tab = mem.alloc("tab", [33, 8], F32)
    zer = mem.alloc("zer", [33, 128], F32)
    ohc = mem.alloc("ohc", [33, 2, 128], F32)
    oh = mem.alloc("oh", [33, LTOT], F32)
    r8 = mem.alloc("r8", [8, LTOT], F32)
    r8d = nc.dram_tensor("r8d", [8, LTOT], F32, kind="Internal")
    ds_oh = newdsem("d_oh")
    ds_oh.inc(nc.sync.dma_start(out=tab[0:32, :], in_=relb.ap()))
    ds_oh.inc(nc.sync.dma_start(out=oh[:], in_=ohd.ap()))
    DVE(nv.memset(tab[32:33, :], NEG))
    t1 = DVE(nv.memset(zer[:], 0.0))
    DVE.wait(ds_oh.total, t1)
    DVE(nv.tensor_scalar(out=ohc[:, 0, :], in0=zer[:], scalar1=oh[:, 1023:1024], scalar2=None, op0=ALU.add))
    t_tr = DVE(nv.tensor_scalar(out=ohc[:, 1, :], in0=zer[:], scalar1=oh[:, OH_FCOL:OH_FCOL + 1], scalar2=None,
                                op0=ALU.add))
    pring0 = Ring(banks[6:8])
    evs = []
    for c0 in range(0, LTOT, 512):
        n = min(512, LTOT - c0)
        bk = pring0.next()
        PE.wait(t_tr, bk.free)
        tk = PE.mark(nc.tensor.matmul(bk.f32[0:8, 0:n], lhsT=tab[:, :], rhs=oh[:, c0:c0 + n], start=True, stop=True))
        DVE.wait(tk)
        te = DVE(nv.tensor_copy(out=r8[:, c0:c0 + n], in_=bk.f32[0:8, 0:n]))
        bk.free = [te]
        evs.append(te)
    bk = pring0.next()
    PE.wait(t_tr, bk.free)
    nc.tensor.matmul(bk.f32[:, 0:8], lhsT=ohc[:, 0, :], rhs=tab[:, :], start=True, stop=True)
    tk = PE.mark(nc.tensor.matmul(bk.f32[:, 8:16], lhsT=ohc[:, 1, :], rhs=tab[:, :], start=True, stop=True))
    DVE.wait(tk)
    DVE(nv.tensor_copy(out=cols[:, C_CFAR:C_CFAR + 8], in_=bk.f32[:, 0:8]))
    te = DVE(nv.tensor_copy(out=cols[:, C_FB:C_FB + 8], in_=bk.f32[:, 8:16]))
    bk.free = [te]
    ds_r = newdsem("d_r")
    POOL.wait(evs)
    t_r8 = ds_r.inc(ng.dma_start(out=r8d.ap(), in_=r8[:]))

    def rscr_broadcast(after_tok):
        POOL.wait(t_r8, after_tok)
        for h in range(8):
            ds_r.inc(ng.dma_start(out=rscr.ap()[h], in_=r8d.ap()[h].partition_broadcast(128)))

    st_tm = mem.alloc("st_tm", [16, 2, 1024], F32)
    stateT = mem.alloc("stateT", [128, 2, 8, 16], F32, at=XBASE + 96 * K)
    t1 = DVE(nv.memset(st_tm[:], 0.0))
    SP.wait(t1)
    ds_st = newdsem("d_st")
    for s in range(2):
        ds_st.inc(nc.sync.dma_start(out=st_tm[1:16, s, :], in_=stp.ap()[s]))
    bk = pring0.next()
    PE.wait(ds_st.total, bk.free, DVE.last)
    tk = None
    for s in range(2):
        for c in range(8):
            tk = nc.tensor.transpose(bk.f32[:, (s * 8 + c) * 16:(s * 8 + c + 1) * 16],
                                     st_tm[0:16, s, c * 128:(c + 1) * 128], ident_f[0:16, 0:16])
    tk = PE.mark(tk)
    DVE.wait(tk)
    te = DVE(nv.tensor_copy(out=stateT[:].rearrange("p s c r -> p (s c r)"), in_=bk.f32[:, 0:256]))
    bk.free = [te]

    gbc = mem.alloc("gbc", [128, D], F32)
    NXS = 4
    xs = [mem.alloc(f"xs{i}", [128, D], F32) for i in range(NXS)]
    xb = [mem.alloc(f"xb{i}", [128, D], BF16) for i in range(2)]
    junk = mem.alloc("junk", [128, D], BF16)
    ds_g = newdsem("d_g")
    t_g = ds_g.inc(nc.sync.dma_start(out=gbc[:], in_=gmix.ap().partition_broadcast(128)))
    ds_x = [newdsem(f"d_x{i}") for i in range(NXS)]
    xs_free = [[] for _ in range(NXS)]
    xb_free = [[], []]
    norm_i = [0]

    def rms_transpose(src, R, g_tok, dstT, col0, src_tok, ring, evac_engs):
        i = norm_i[0]
        norm_i[0] += 1
        c_ssq = C_SSQ + 3 * i
        ssq = cols[0:R, c_ssq:c_ssq + 1]
        rt = cols[0:R, c_ssq + 1:c_ssq + 2]
        rstd = cols[0:R, c_ssq + 2:c_ssq + 3]
        ACT.wait(src_tok)
        t_sq = ACT(ns.activation(out=junk[0:R, :], in_=src, func=AF.Square, accum_out=ssq))
        ACT.wait(t_sq)
        t_rt = ACT(ns.activation(out=rt, in_=ssq, func=AF.Sqrt, bias=cols[0:R, C_EPS:C_EPS + 1],
                                        scale=1.0 / D))
        DVE.wait(t_rt)
        t_rs = DVE(nv.reciprocal(out=rstd, in_=rt))
        xbi = xb[i % 2]
        DVE.wait(t_rs, g_tok, src_tok, xb_free[i % 2])
        t_xb = DVE(nv.scalar_tensor_tensor(out=xbi[0:R, :], in0=src, scalar=rstd, in1=gbc[0:R, :],
                                                  op0=ALU.mult, op1=ALU.mult))
        pend_ev = []
        for half in range(2):
            bk = ring.next()
            PE.wait(t_xb, bk.free)
            tk = None
            for cc in range(8):
                c = half * 8 + cc
                tk = nc.tensor.transpose(bk.bf[:, cc * 128:cc * 128 + R], xbi[0:R, c * 128:(c + 1) * 128],
                                         ident_b[0:R, 0:R])
            pend_ev.append((bk, PE.mark(tk), half))
        xb_free[i % 2] = [PE.last]

        def back():
            evs = []
            for (bk, tk, half) in pend_ev:
                E = evac_engs[half % len(evac_engs)]
                E.wait(tk)
                src_v = bk.bf.rearrange("p (c r) -> p c r", r=128)[:, :, 0:R]
                dst_v = dstT[:, half * 8:half * 8 + 8, col0:col0 + R]
                if E is ACT:
                    te = ACT(ns.copy(out=dst_v, in_=src_v))
                else:
                    te = E(E.px.tensor_copy(out=dst_v, in_=src_v))
                bk.free = [te]
                evs.append(te)
            return evs
        return [t_sq, t_xb], back

    tilesA = [(r0, 128) for r0 in range(0, 2048, 128)] + [(2048, 32), (2080, 128)]
    ringA = Ring(banks[0:6])
    xnT_done = []
    pend_back = []
    for i, (r0, R) in enumerate(tilesA):
        sl = i % NXS
        SP.wait(xs_free[sl])
        t_ld = ds_x[sl].inc(nc.sync.dma_start(out=xs[sl][0:R, :], in_=xin.ap()[r0:r0 + R, :]))
        if i == 13:
            rscr_broadcast(t_ld)
            t_rscr = [ds_r.total]
        rel, back = rms_transpose(xs[sl][0:R, :], R, t_g, xnT, r0, t_ld, ringA, [ACT, DVE])
        xs_free[sl] = rel
        if pend_back:
            xnT_done += pend_back.pop()()
        pend_back.append(back)
    xnT_done += pend_back.pop()()

    if "xnT" in dbg:
        d = dbg_out("xnT", [128, 16 * NTOK])
        tmpf = mem.alloc("dbgf", [128, NTOK], F32)
        ds_dbg = newdsem("d_dbg")
        for c in range(16):
            DVE.wait(xnT_done, ds_dbg.total)
            t1 = DVE(nv.tensor_copy(out=tmpf[:], in_=xnT[:, c, :]))
            SP.wait(t1)
            ds_dbg.inc(nc.sync.dma_start(out=d.ap()[:, c * NTOK:(c + 1) * NTOK], in_=tmpf[:]))

    if stop_after == "A":
        barrier()
        return nc, dbg_t

    barrier()
    mem.ptr = p0_mark
    NW = 3
    wslots = [Slot(mem.alloc(f"w{i}", [128, 16, 128], BF16), newdsem(f"d_w{i}")) for i in range(NW)]
    w_in_v = w_in.ap().rearrange("(c p) n -> p c n", p=128)
    wsrc = [w_in_v[:, :, j * 128:(j + 1) * 128] for j in range(8)]
    for h in range(8):
        for s3 in range(3):
            co = 1024 + s3 * 1024 + h * 128
            wsrc.append(w_in_v[:, :, co:co + 128])
    w_issued = [0]

    def w_issue():
        i = w_issued[0]
        if i >= len(wsrc):
            return
        sl = wslots[i % NW]
        POOL.wait(sl.free)
        sl.ready = sl.ds.inc(ng.dma_start(out=sl.t[:], in_=wsrc[i]))
        w_issued[0] += 1

    def w_get(i):
        assert i < w_issued[0]
        return wslots[i % NW]

    def w_release(i, tok):
        wslots[i % NW].free = [tok]
        w_issue()

    wp = mem.alloc("wp", [128, 4, 2, 256], BF16)
    ds_wp = newdsem("d_wp")
    t_wp = ds_wp.inc(ng.dma_start(out=wp[:], in_=w_pool.ap().rearrange("g (cc p) d -> p g cc d", p=128)))
    for _ in range(NW):
        w_issue()

    ubuf = [mem.alloc(f"u{i}", [128, UW], F32) for i in range(3)]
    utmp = [mem.alloc(f"ut{i}", [128, UW], F32) for i in range(2)]
    pbuf = [mem.alloc(f"pb{i}", [128, UW], BF16) for i in range(2)]
    pst = [mem.alloc(f"pst{i}", [16, 384], F32) for i in range(2)]
    ds_pst = [newdsem("d_pst0"), newdsem("d_pst1")]
    ut_init = [DVE(nv.memset(utmp[i][:], 0.0)) for i in range(2)]
    ds_out = newdsem("d_out")
    u_free = [[], [], []]
    ut_free = [[ut_init[0]], [ut_init[1]]]
    pb_free = [[], []]
    pst_tok = [None, None]
    pb_ready = [None, None]
    ringU = Ring(banks)
    WIN = (2, 4, 8, 16)
    abT_a_done = []
    p_o_v = p_o.ap().rearrange("(g r) n -> r g n", r=16)
    u_evs = {}

    def u_main(j):
        sl = w_get(j)
        u = ubuf[j % 3]
        evs = []
        tk_last = None
        for (c0, n, kind) in ((0, 512, 0), (512, 512, 1), (C_HALO, 144, 2)):
            bk = ringU.next()
            PE.wait(sl.ready, xnT_done, bk.free)
            for c in range(16):
                tk = nc.tensor.matmul(bk.f32[:, 0:n], lhsT=sl.t[:, c, :], rhs=xnT[:, c, c0:c0 + n],
                                      start=(c == 0), stop=(c == 15))
            tk_last = PE.mark(tk)
            ACT.wait(tk_last, u_free[j % 3])
            if kind < 2:
                te = ACT(ns.copy(out=u[:, 16 + c0:16 + c0 + 512], in_=bk.f32[:, 0:512]))
                evs.append(te)
            else:
                evs.append(ACT(ns.copy(out=u[:, 0:16], in_=bk.f32[:, 0:16])))
                uv = u[:, 1040:1200].rearrange("p (s c) -> p s c", c=80)
                te = ACT(ns.copy(out=uv[:, :, 16:80],
                                 in_=bk.f32[:, 16:144].rearrange("p (s c) -> p s c", c=64)))
                evs.append(te)
                evs.append(ACT(ns.copy(out=uv[:, :, 0:16], in_=stateT[:, :, j, :])))
            bk.free = [te]
        w_release(j, tk_last)
        u_evs[j] = evs

    def u_mix(j):
        g, cc = j // 2, j % 2
        u = ubuf[j % 3]
        evs = u_evs.pop(j)
        bk = ringU.next()
        PE.wait(evs, bk.free)
        for i3, a in enumerate((1024, 1104, 1184)):
            tk = nc.tensor.transpose(bk.f32[0:16, i3 * 128:(i3 + 1) * 128], u[:, a:a + 16], ident_f[:, :])
        tk = PE.mark(tk)
        DVE.wait(tk, pst_tok[j % 2])
        te = DVE(nv.tensor_copy(out=pst[j % 2][:, :], in_=bk.f32[0:16, 0:384]))
        bk.free = [te]
        SP.wait(te)
        pst_tok[j % 2] = ds_pst[j % 2].inc(nc.sync.dma_start(
            out=p_o_v[:, :, j * 128:(j + 1) * 128], in_=pst[j % 2][:, :].rearrange("r (g n) -> r g n", n=128)))
        a_cur = u
        DVE.wait(evs)
        for st in range(g + 1):
            sh = 1 << st
            nxt = utmp[st % 2]
            DVE.wait(ut_free[st % 2])
            DVE(nv.tensor_tensor(out=nxt[:, sh:UW], in0=a_cur[:, sh:UW], in1=a_cur[:, 0:UW - sh], op=ALU.add))
            a_cur = nxt
        DVE.wait(pb_free[cc])
        t_p = DVE(nv.scalar_tensor_tensor(out=pbuf[cc][:, 16:UW], in0=a_cur[:, 16:UW], scalar=1.0 / WIN[g],
                                          in1=u[:, 16:UW], op0=ALU.mult, op1=ALU.subtract))
        u_free[j % 3] = [t_p, tk]
        ut_free[0] = [t_p]
        ut_free[1] = [t_p]
        pb_ready[cc] = t_p

    def u_out(g):
        for dc in range(2):
            ch = 2 * g + dc
            for (src0, n, dst0) in ((16, 512, 0), (528, 512, 512), (1056, 64, 1024), (1136, 64, 1088)):
                bk = ringU.next()
                PE.wait(pb_ready, t_wp, bk.free)
                for c2 in range(2):
                    tk = nc.tensor.matmul(bk.f32[:, 0:n], lhsT=wp[:, g, c2, dc * 128:(dc + 1) * 128],
                                          rhs=pbuf[c2][:, src0:src0 + n], start=(c2 == 0), stop=(c2 == 1))
                tk = PE.mark(tk)
                ACT.wait(tk)
                te = ACT(ns.activation(out=abT[:, ch, dst0:dst0 + n], in_=bk.f32[:, 0:n], func=AF.Copy,
                                       scale=cols[:, C_PSC + ch:C_PSC + ch + 1]))
                bk.free = [te]
                abT_a_done.append(te)
        pb_free[0] = [PE.last]
        pb_free[1] = [PE.last]

    u_main(0)
    for j in range(1, 9):
        if j < 8:
            u_main(j)
        u_mix(j - 1)
        if (j - 1) % 2 == 1:
            u_out((j - 1) // 2)

    def dump_bf(name, src3, nch, ncols, deps, at=200 * K, ntot=None):
        d = dbg_out(name, [128, (ntot or nch) * ncols])
        tmpf = mem.alloc("dbgf_" + name, [128, ncols], F32, at=at)
        dsd = newdsem("d_dbg_" + name)
        for c in range(nch):
            DVE.wait(deps, dsd.total)
            t1 = DVE(nv.tensor_copy(out=tmpf[:], in_=src3(c)))
            SP.wait(t1)
            dsd.inc(nc.sync.dma_start(out=d.ap()[:, c * ncols:(c + 1) * ncols], in_=tmpf[:]))

    if "abTa" in dbg:
        dump_bf("abTa", lambda c: abT[:, c, :], 8, NQ, abT_a_done)

    if stop_after == "U":
        barrier()
        return nc, dbg_t

    barrier()
    att_mark = mem.ptr = p0_mark + NW * 4096
    QT2 = [mem.alloc(f"QT{i}", [128, 2, NQ], BF16) for i in range(2)]
    KT2 = [mem.alloc(f"KT{i}", [128, NTOK], BF16) for i in range(2)]
    Vb = mem.alloc("Vb", [128, 18, 128], BF16)
    kst = [Slot(mem.alloc(f"kst{i}", [128, 4, 128], F32), newdsem(f"d_kst{i}")) for i in range(3)]
    kst_i = [0]
    bown = mem.alloc("bown", [128, 5, 512], F32)
    bf7 = mem.alloc("bf7", [128, 512], F32)
    bmeta = mem.alloc("bmeta", [16, 512], F32)
    bs15 = mem.alloc("bs15", [128, 2, 64], F32)
    bs16 = mem.alloc("bs16", [16, 2, 64], F32)
    bsn = mem.alloc("bsn", [128, 2, 64], F32)
    ds_b = newdsem("d_bias")
    cslots = []
    for i in range(2):
        cslots.append(dict(kcT=mem.alloc(f"kcT{i}", [128, 2064], BF16),
                           vc=mem.alloc(f"vc{i}", [128, 17, 128], BF16), ds=newdsem(f"d_c{i}"), free=[], ready=None))
    ktm = mem.alloc("ktm", [128, 17, 128], BF16)
    ktm_ds = newdsem("d_ktm")
    ktm_free = []
    Qbd = [mem.alloc(f"qbd{i}", [128, 128], BF16) for i in range(2)]
    PT = [mem.alloc(f"pt{i}", [128, 512], BF16) for i in range(4)]
    pt_free = [[] for _ in range(4)]
    pt_i = [0]
    PTs = mem.alloc("pts", [128, 18, 128], BF16)
    tmpn = [mem.alloc(f"tmpn{i}", [128, 512], F32) for i in range(2)]
    tmpn_free = [[], []]
    tmpn_i = [0]
    post_t = [mem.alloc(f"post_t{i}", [128, 512], F32) for i in range(2)]
    post_r = [mem.alloc(f"post_r{i}", [128, 512], F32) for i in range(2)]
    post_free = [[], []]
    spost_t = mem.alloc("spost_t", [128, 128], F32)
    spost_r = mem.alloc("spost_r", [128, 128], F32)
    spost_q = mem.alloc("spost_q", [128, 64], BF16)
    post_i = [0]
    t1 = DVE(nv.memset(Qbd[0][:], 0.0))
    t2 = DVE(nv.memset(Qbd[1][:], 0.0))
    DVE(nv.memset(QT2[0][:].rearrange("p a b -> p (a b)"), 0.0))
    t_qz = DVE(nv.memset(QT2[1][:].rearrange("p a b -> p (a b)"), 0.0))
    qbd_free = [[t1], [t2]]
    acc = banks[0:4]
    SAcc = banks[4]
    ringS = Ring(banks[5:8])
    ringP = Ring(banks[0:8])
    evac_rr = [0]
    qkv_free = []
    bias_free = []
    pts_free = []
    abT_b_done = []
    ck_v = ck.ap()
    cv_v = cv.ap()

    def evac_copy(dst, src, waits):
        evac_rr[0] += 1
        if evac_rr[0] % 2:
            ACT.wait(waits)
            return ACT(ns.copy(out=dst, in_=src))
        DVE.wait(waits)
        return DVE(nv.tensor_copy(out=dst, in_=src))

    def proj_T_gen(sl, groups, dstT, is_q, dve_only, ring, wait_free):
        res = {"last": None, "toks": []}
        for (c0, n, dst0) in groups:
            bk = ring.next()
            PE.wait(sl.ready, bk.free, wait_free)
            for c in range(16):
                tk = nc.tensor.matmul(bk.f32[:, 0:n], lhsT=sl.t[:, c, :], rhs=xnT[:, c, c0:c0 + n],
                                      start=(c == 0), stop=(c == 15))
            last = PE.mark(tk)
            res["last"] = last
            if is_q:
                if dve_only:
                    DVE.wait(last, t_qz)
                    ta_ = DVE(nv.tensor_copy(out=dstT[0:64, 0, dst0:dst0 + n], in_=bk.f32[0:64, 0:n]))
                    te = DVE(nv.tensor_copy(out=dstT[64:128, 1, dst0:dst0 + n], in_=bk.f32[64:128, 0:n]))
                else:
                    ACT.wait(last, t_qz)
                    ta_ = ACT(ns.copy(out=dstT[0:64, 0, dst0:dst0 + n], in_=bk.f32[0:64, 0:n]))
                    te = ACT(ns.copy(out=dstT[64:128, 1, dst0:dst0 + n], in_=bk.f32[64:128, 0:n]))
                res["toks"].append(ta_)
            elif dve_only:
                DVE.wait(last)
                te = DVE(nv.tensor_copy(out=dstT[:, dst0:dst0 + n], in_=bk.f32[:, 0:n]))
            else:
                te = evac_copy(dstT[:, dst0:dst0 + n], bk.f32[:, 0:n], [last])
            bk.free = [te]
            res["toks"].append(te)
            yield res
        yield res

    def kst_next():
        sl = kst[kst_i[0] % 3]
        kst_i[0] += 1
        return sl

    def proj_tok_gen(sl, h, out_dram, vdst, dve_only, ring, wait_free):
        toks = []
        last = None
        res = {"last": None, "toks": toks}

        def st_copy(out, in_, waits):
            if dve_only:
                DVE.wait(waits)
                return DVE(nv.tensor_copy(out=out, in_=in_))
            ACT.wait(waits)
            return ACT(ns.copy(out=out, in_=in_))
        tile_groups = [(0, 0, True), (4, 512, True)]
        if vdst:
            tile_groups += [(8, 1024, False), (12, 1536, False)]
        for (slot0, c0, is_out) in tile_groups:
            bk = ring.next()
            PE.wait(sl.ready, bk.free, wait_free)
            for tt in range(4):
                for c in range(16):
                    tk = nc.tensor.matmul(bk.f32[:, tt * 128:(tt + 1) * 128],
                                          lhsT=xnT[:, c, c0 + tt * 128:c0 + (tt + 1) * 128], rhs=sl.t[:, c, :],
                                          start=(c == 0), stop=(c == 15))
            last = PE.mark(tk)
            rd = []
            if is_out:
                ks = kst_next()
                te = st_copy(ks.t[:].rearrange("p a b -> p (a b)"), bk.f32[:, :], [last, ks.ds.total, ks.free])
                rd.append(te)
                SP.wait(te)
                dst = bass.AP(out_dram, c0 * 1024 + h * 128, [[1024, 128], [128 * 1024, 4], [1, 128]])
                ks.ds.inc(nc.sync.dma_start(out=dst, in_=ks.t[:]))
                if vdst:
                    DVE.wait(te)
                    tp_ = DVE(nv.tensor_copy(out=Vb[:, slot0:slot0 + 4, :], in_=ks.t[:]))
                    ks.free = [tp_]
                    toks.append(tp_)
            elif vdst:
                DVE.wait(last)
                te = DVE(nv.tensor_copy(out=Vb[:, slot0:slot0 + 4, :].rearrange("p a b -> p (a b)"),
                                               in_=bk.f32[:, :]))
                rd.append(te)
                toks.append(te)
            bk.free = rd
            res["last"] = last
            yield res
        bk = ring.next()
        PE.wait(sl.ready, bk.free, wait_free)
        for c in range(16):
            nc.tensor.matmul(bk.f32[0:16, 0:128], lhsT=xnT[:, c, C_META:C_META + 16], rhs=sl.t[:, c, :],
                             start=(c == 0), stop=(c == 15))
        for c in range(16):
            tk = nc.tensor.matmul(bk.f32[:, 128:256], lhsT=xnT[:, c, C_SAMP:C_SAMP + 128], rhs=sl.t[:, c, :],
                                  start=(c == 0), stop=(c == 15))
        last = PE.mark(tk)
        rd = []
        ks = kst_next()
        ta = st_copy(ks.t[0:16, 0, :], bk.f32[0:16, 0:128], [last, ks.ds.total, ks.free])
        tb = st_copy(ks.t[:, 1, :], bk.f32[:, 128:256], [last])
        rd += [ta, tb]
        if vdst:
            DVE.wait(ta, tb)
            t1_ = DVE(nv.tensor_copy(out=Vb[0:16, 16, :], in_=ks.t[0:16, 0, :]))
            t2_ = DVE(nv.tensor_copy(out=Vb[:, 17, :], in_=ks.t[:, 1, :]))
            ks.free = [t1_, t2_]
            toks += [t1_, t2_]
        SP.wait(ta, tb)
        ks.ds.inc(nc.sync.dma_start(out=out_dram.ap()[1024:1040, h, :], in_=ks.t[0:16, 0, :]))
        ks.ds.inc(nc.sync.dma_start(out=out_dram.ap()[1040:1168, h, :], in_=ks.t[:, 1, :]))
        bk.free = rd
        res["last"] = last
        yield res

    def run_gen(g):
        r = None
        for r in g:
            pass
        return r

    last_bias_read = [None]

    def softmax_unit(bk, nk, p0, kind, barg, pt_ap, s_tok):
        src = bk
        if kind == "far":
            ACT.wait(s_tok)
            return ACT(ns.activation(out=pt_ap, in_=src, func=AF.Exp, bias=barg, scale=ATT_SCALE)), None
        ti = tmpn_i[0] % 2
        tmpn_i[0] += 1
        n = src.shape[-1]
        tm = tmpn[ti][p0:p0 + nk, 0:n]
        DVE.wait(s_tok, tmpn_free[ti])
        td = DVE(nv.scalar_tensor_tensor(out=tm, in0=src, scalar=ATT_SCALE, in1=barg,
                                                op0=ALU.mult, op1=ALU.add))
        last_bias_read[0] = td
        ACT.wait(td)
        te = ACT(ns.activation(out=pt_ap, in_=tm, func=AF.Exp))
        tmpn_free[ti] = [te]
        return te, td

    def post_process(O0, O1, D0, D1, n, h, dst_ap, pv_tok, acc_banks):
        O0s, D0s, O1s, D1s = post_t[0][:, 0:n], post_r[0][:, 0:n], post_t[1][:, 0:n], post_r[1][:, 0:n]
        ACT.wait(pv_tok, post_free[0])
        c1 = ACT(ns.copy(out=D0s, in_=D0))
        c2 = ACT(ns.copy(out=D1s, in_=D1))
        DVE.wait(pv_tok, post_free[0])
        c3 = DVE(nv.tensor_copy(out=O0s, in_=O0))
        c4 = DVE(nv.tensor_copy(out=O1s, in_=O1))
        for b_ in acc_banks:
            b_.free = [c1, c2, c3, c4]
        DVE.wait(c1, c2)
        DVE(nv.reciprocal(out=D0s, in_=D0s))
        DVE(nv.tensor_tensor(out=O0s, in0=O0s, in1=D0s, op=ALU.mult))
        DVE(nv.reciprocal(out=D1s, in_=D1s))
        DVE(nv.tensor_tensor(out=O1s, in0=O1s, in1=D1s, op=ALU.mult))
        tc_ = DVE(nv.scalar_tensor_tensor(out=O0s, in0=O1s, scalar=cols[:, C_NEGLAM:C_NEGLAM + 1], in1=O0s,
                                          op0=ALU.mult, op1=ALU.add))
        sqb = post_r[0][:, 0:n].bitcast(BF16)[:, 0:n]
        tsq = DVE(nv.tensor_tensor(out=sqb, in0=O0s, in1=O0s, op=ALU.mult))

        def part2():
            bk = ringS.next()
            PE.wait(tsq, bk.free)
            tk = PE.mark(nc.tensor.matmul(bk.f32[:, 0:n], lhsT=ones_b[:, :], rhs=sqb, start=True, stop=True))
            ACT.wait(tk)
            tl = ACT(ns.activation(out=D1s, in_=bk.f32[:, 0:n], func=AF.Ln, bias=cols[:, C_SEPS:C_SEPS + 1]))
            bk.free = [tl]
            te = ACT(ns.activation(out=D1s, in_=D1s, func=AF.Exp, scale=-0.5))
            DVE.wait(te)
            tf = DVE(nv.scalar_tensor_tensor(out=dst_ap, in0=O0s, scalar=cols[:, C_GCOL:C_GCOL + 1], in1=D1s,
                                             op0=ALU.mult, op1=ALU.mult))
            post_free[0] = [tf]
            abT_b_done.append(tf)
        return part2

    carry = []
    hres = mem.alloc("hres", [128, 9, D], F32, at=39 * K)
    ds_h = newdsem("d_hres")
    wo = [Slot(mem.alloc(f"wo{i}", [128, 16, 256], BF16, at=111 * K + i * 8 * K), newdsem(f"d_wo{i}")) for i in range(4)]
    w_out_v = w_out.ap().rearrange("(c p) n -> p c n", p=128)
    buf_free = [[], []]
    qk_ready = {}

    def inproj_qk_gen(hh, dve_only, ring):
        par = hh % 2
        wi_ = 8 + 3 * hh
        toks = []
        sl_ = w_get(wi_)
        r = None
        for r in proj_T_gen(sl_, [(0, 512, 0), (512, 512, 512), (C_SAMP, 128, 1024)], QT2[par], True, dve_only, ring,
                            buf_free[par]):
            yield
        toks += r["toks"]
        w_release(wi_, r["last"])
        sl_ = w_get(wi_ + 1)
        for r in proj_T_gen(sl_, [(0, 512, 0), (512, 512, 512), (1024, 512, 1024), (1536, 512, 1536),
                                  (2048, 160, 2048)], KT2[par], False, dve_only, ring, buf_free[par]):
            yield
        toks += r["toks"]
        for r in proj_tok_gen(sl_, hh, k_o, False, dve_only, ring, []):
            yield
        w_release(wi_ + 1, r["last"])
        qk_ready[hh] = toks

    run_gen(inproj_qk_gen(0, False, ringP))

    for h in range(nheads):
        QT, KT = QT2[h % 2], KT2[h % 2]
        rb = h * 128 * LTOT
        SP.wait(t_rscr, bias_free)
        ds_b.inc(nc.sync.dma_start(out=bown[:], in_=bass.AP(rscr, rb + 127, [[LTOT - 1, 128], [128, 5], [1, 512]])))
        ds_b.inc(nc.sync.dma_start(out=bf7[:], in_=bass.AP(rscr, rb + OH_F7 + 127, [[LTOT - 1, 128], [1, 512]])))
        ds_b.inc(nc.sync.dma_start(out=bmeta[:], in_=bass.AP(rscr, rb + OH_META + 15, [[LTOT - 1, 16], [1, 512]])))
        for m_ in range(2):
            ds_b.inc(nc.sync.dma_start(out=bs15[:, m_, :], in_=bass.AP(rscr, rb + 655, [[LTOT - 1, 128], [1, 64]])))
            ds_b.inc(nc.sync.dma_start(out=bs16[:, m_, :], in_=bass.AP(rscr, rb + 527, [[LTOT - 1, 16], [1, 64]])))
            for s_ in range(2):
                ds_b.inc(nc.sync.dma_start(out=bsn[64 * s_:64 * s_ + 64, m_, :],
                                           in_=bass.AP(rscr, rb + 511, [[LTOT - 1, 64], [1, 64]])))
        t_bias_ld = ds_b.total
        POOL.wait(t_bias_ld)
        t_mask = None
        for tp in range(4):
            t = 3 - tp
            for ph in range(2):
                if 2 * t + ph > 0:
                    t_mask = POOL(ng.memset(bown[64 * ph:64 * ph + 64, tp, 0:64 * (2 * t + ph)], NEG))
        t_bias = [t_bias_ld, t_mask]
        def load_cache_k(s):
            POOL.wait(ktm_free)
            ktm_ds.inc(ng.dma_start(out=ktm[:, 0:16, :],
                                    in_=ck_v[s, 0:2048, h, :].rearrange("(t p) d -> p t d", p=128)))
            return ktm_ds.inc(ng.dma_start(out=ktm[0:16, 16, :], in_=ck_v[s, 2048:2064, h, :]))

        ktm_ready = {0: load_cache_k(0)}
        for s in range(2):
            cs = cslots[s]
            POOL.wait(cs["free"])
            cs["ds"].inc(ng.dma_start(out=cs["vc"][:, 0:16, :],
                                      in_=cv_v[s, 0:2048, h, :].rearrange("(t p) d -> p t d", p=128)))
            cs["ds"].inc(ng.dma_start(out=cs["vc"][0:16, 16, :], in_=cv_v[s, 2048:2064, h, :]))
            cs["ready"] = cs["ds"].total
        wi = 8 + 3 * h
        sl = w_get(wi + 2)
        r = run_gen(proj_tok_gen(sl, h, v_o, True, False, ringP, []))
        lastv, tv = r["last"], r["toks"]
        w_release(wi + 2, lastv)
        if carry:
            carry.pop()()
        qkv_ready = qk_ready[h] + tv
        if h == nheads - 1:
            SP.wait(lastv, DVE.last, ACT.last)
            ds_h.inc(nc.sync.dma_start(out=hres[:, 0:8, :],
                                       in_=xin.ap()[0:1024, :].rearrange("(t p) d -> p t d", p=128)))
            ds_h.inc(nc.sync.dma_start(out=hres[:, 8, :], in_=xin.ap()[C_SAMP:C_SAMP + 128, :]))
            POOL.wait(lastv)
            wo[0].ready = wo[0].ds.inc(ng.dma_start(out=wo[0].t[:], in_=w_out_v[:, :, 0:256]))

        samp_prep = {}

        def sample_prep(s):
            nonlocal ktm_free
            cs = cslots[s]
            qb = Qbd[s]
            DVE.wait(qkv_ready, qbd_free[s])
            DVE(nv.tensor_copy(out=qb[0:64, 0:64], in_=QT[0:64, 0, 1024 + 64 * s:1088 + 64 * s]))
            t_qbd = DVE(nv.tensor_copy(out=qb[64:128, 64:128], in_=QT[64:128, 1, 1024 + 64 * s:1088 + 64 * s]))
            t_kcT = []
            for g3 in range(3):
                bk = ringS.next()
                PE.wait(ktm_ready[s], bk.free, cs["free"])
                if g3 < 2:
                    for i in range(8):
                        tk = nc.tensor.transpose(bk.bf[:, i * 128:(i + 1) * 128], ktm[:, g3 * 8 + i, :], ident_b[:, :])
                    n = 1024
                else:
                    tk = nc.tensor.transpose(bk.bf[:, 0:16], ktm[0:16, 16, :], ident_b[0:16, 0:16])
                    n = 16
                tk = PE.mark(tk)
                DVE.wait(tk)
                te = DVE(nv.tensor_copy(out=cs["kcT"][:, g3 * 1024:g3 * 1024 + n], in_=bk.bf[:, 0:n]))
                bk.free = [te]
                t_kcT.append(te)
            ktm_free = [PE.last]
            samp_prep[s] = (t_qbd, t_kcT)

        def sample_gen():
            nonlocal pts_free
            for s in range(2):
                cs = cslots[s]
                qb = Qbd[s]
                p0 = 64 * s
                if s == 1:
                    sample_prep(1)
                    yield
                t_qbd, t_kcT = samp_prep[s]
                pts_toks = []
                for (c0, ncnt) in ((0, 4), (4, 4), (8, 4), (12, 3)):
                    bk = ringS.next()
                    PE.wait(t_kcT, t_qbd, bk.free)
                    for i in range(ncnt):
                        c = c0 + i
                        tk = nc.tensor.matmul(bk.f32[:, i * 128:(i + 1) * 128],
                                              lhsT=cs["kcT"][:, c * 128:(c + 1) * 128], rhs=qb[:, :],
                                              start=True, stop=True)
                    tk = PE.mark(tk)
                    ACT.wait(tk, pts_free)
                    te = ACT(ns.activation(out=PTs[:, c0:c0 + ncnt, :].rearrange("p a b -> p (a b)"),
                                           in_=bk.f32[:, 0:ncnt * 128], func=AF.Exp,
                                           bias=cols[:, C_CFAR + h:C_CFAR + h + 1], scale=ATT_SCALE))
                    bk.free = [te]
                    pts_toks.append(te)
                    yield
                bk = ringS.next()
                PE.wait(t_kcT, t_qbd, bk.free, qkv_ready)
                nc.tensor.matmul(bk.f32[:, 0:128], lhsT=cs["kcT"][:, 1920:2048], rhs=qb[:, :], start=True, stop=True)
                nc.tensor.matmul(bk.f32[0:16, 128:256], lhsT=cs["kcT"][:, 2048:2064], rhs=qb[:, :],
                                 start=True, stop=True)
                tk = PE.mark(nc.tensor.matmul(bk.f32[p0:p0 + 64, 256:384],
                                              lhsT=KT[:, C_SAMP + 64 * s:C_SAMP + 64 * s + 64], rhs=qb[:, :],
                                              start=True, stop=True))
                DVE.wait(t_bias)
                ACT.wait(pts_free)
                rds = []
                for (pa, nk_, col0, bt, slot) in ((0, 128, 0, bs15, 15), (0, 16, 128, bs16, 16), (p0, 64, 256, bsn, 17)):
                    te, td = softmax_unit(bk.f32[pa:pa + nk_, col0:col0 + 128], nk_, pa, "near",
                                          bt[pa:pa + nk_, :, :].rearrange("p a b -> p (a b)"),
                                          PTs[pa:pa + nk_, slot, :], tk)
                    rds.append(td)
                    pts_toks.append(te)
                bk.free = rds
                yield
                PE.wait(pts_toks, SAcc.free, cs["ready"])
                order = [17] + list(range(0, 8)) + [16] + list(range(8, 16))
                for which in range(2):
                    for oi, c in enumerate(order):
                        if c == 17:
                            pa, nk_ = p0, 64
                            lhs = Vb[pa:pa + 64, 17, :] if which == 0 else ones_b[pa:pa + 64, :]
                        elif c == 16:
                            pa, nk_ = 0, 16
                            lhs = cs["vc"][0:16, 16, :] if which == 0 else ones_b[0:16, :]
                        else:
                            pa, nk_ = 0, 128
                            lhs = cs["vc"][:, c, :] if which == 0 else ones_b[:, :]
                        if which == 0:
                            tk = nc.tensor.matmul(SAcc.f32[:, 0:128], lhsT=lhs, rhs=PTs[pa:pa + nk_, c, :],
                                                  start=(oi == 0), stop=(oi == len(order) - 1))
                        else:
                            tk = nc.tensor.matmul(SAcc.f32[:, 128:256], lhsT=lhs, rhs=PTs[pa:pa + nk_, c, :],
                                                  start=False, stop=(oi == len(order) - 1), skip_group_check=True)
                        if oi % 5 == 4 and oi < len(order) - 1:
                            yield
                    if which == 0:
                        yield
                tk_pv = PE.mark(tk)
                pts_free = [tk_pv]
                cs["free"] = [tk_pv]
                qbd_free[s] = [tk_pv]
                yield
                pi = 1
                t0 = spost_t[:, 0:128]
                r = spost_r[:, 0:128]
                DVE.wait(tk_pv, post_free[pi])
                DVE(nv.reciprocal(out=r, in_=SAcc.f32[:, 128:256]))
                tb = DVE(nv.tensor_tensor(out=t0, in0=SAcc.f32[:, 0:128], in1=r, op=ALU.mult))
                SAcc.free = [tb]
                tc_ = DVE(nv.scalar_tensor_tensor(out=t0[:, 0:64], in0=t0[:, 64:128],
                                                  scalar=cols[:, C_NEGLAM:C_NEGLAM + 1], in1=t0[:, 0:64],
                                                  op0=ALU.mult, op1=ALU.add))
                sqb = spost_q[:, 0:64]
                ACT.wait(tc_)
                tsq = ACT(ns.activation(out=sqb, in_=t0[:, 0:64], func=AF.Square))
                yield
                bk = ringS.next()
                PE.wait(tsq, bk.free)
                tk = PE.mark(nc.tensor.matmul(bk.f32[:, 0:64], lhsT=ones_b[:, :], rhs=sqb, start=True, stop=True))
                ACT.wait(tk)
                tl = ACT(ns.activation(out=r[:, 0:64], in_=bk.f32[:, 0:64], func=AF.Ln,
                                       bias=cols[:, C_SEPS:C_SEPS + 1]))
                bk.free = [tl]
                te = ACT(ns.activation(out=r[:, 0:64], in_=r[:, 0:64], func=AF.Exp, scale=-0.5))
                yield
                DVE.wait(te)
                tf = DVE(nv.scalar_tensor_tensor(out=abT[:, 8 + h, 1024 + 64 * s:1088 + 64 * s], in0=t0[:, 0:64],
                                                 scalar=cols[:, C_GCOL:C_GCOL + 1], in1=r[:, 0:64],
                                                 op0=ALU.mult, op1=ALU.mult))
                post_free[pi] = [tf]
                abT_b_done.append(tf)
                yield

        sample_prep(0)
        ktm_ready[1] = load_cache_k(1)
        filler = sample_gen()
        nxt = inproj_qk_gen(h + 1, False, ringS) if h + 1 < nheads else None

        def pull_nxt():
            if nxt is not None:
                try:
                    next(nxt)
                except StopIteration:
                    pass

        def pull(n=1):
            for _ in range(n):
                try:
                    next(filler)
                except StopIteration:
                    return False
            return True

        deferred = []
        for B in range(2):
            q0 = 512 * B
            units = []
            if B == 0:
                units.append((C_META, 16, 16, "near", bmeta[0:16, :]))
                for t in range(8):
                    if t == 7:
                        units.append((C_FOR + 128 * t, 128, 8 + t, "near", bf7[:, :]))
                    else:
                        units.append((C_FOR + 128 * t, 128, 8 + t, "far", cols[:, C_FB + h:C_FB + h + 1]))
                for t in range(4):
                    units.append((128 * t, 128, t, "near", bown[:, 3 - t, :]))
            else:
                units.append((C_META, 16, 16, "far", cols[0:16, C_CFAR + h:C_CFAR + h + 1]))
                for t in range(8):
                    units.append((C_FOR + 128 * t, 128, 8 + t, "far", cols[:, C_FB + h:C_FB + h + 1]))
                for t in range(3):
                    units.append((128 * t, 128, t, "far", cols[:, C_CFAR + h:C_CFAR + h + 1]))
                units.append((128 * 3, 128, 3, "near", bown[:, 4, :]))
                for t in range(4, 8):
                    units.append((128 * t, 128, t, "near", bown[:, 3 - (t - 4), :]))
            near_u = [u_ for u_ in units if u_[3] == "near"]
            far_u = [u_ for u_ in units if u_[3] != "near"]
            lead = far_u[:2] if B == 0 else far_u[:6]
            rest_far = far_u[len(lead):]
            mixed = []
            while near_u or rest_far:
                if rest_far:
                    mixed.append(rest_far.pop(0))
                if near_u:
                    mixed.append(near_u.pop(0))
            units = lead + mixed
            steps = [(u, m_) for u in range(len(units)) for m_ in range(2)]
            pend = {}

            def stageA(u, m_):
                kc0, nk, vslot, kind, barg = units[u]
                bk = ringS.next()
                PE.wait(qkv_ready, bk.free)
                tk = PE.mark(nc.tensor.matmul(bk.f32[0:nk, :], lhsT=KT[:, kc0:kc0 + nk],
                                              rhs=QT[:, m_, q0:q0 + 512], start=True, stop=True))
                pi = pt_i[0] % 4
                pt_i[0] += 1
                if kind != "far":
                    DVE.wait(t_bias)
                ACT.wait(pt_free[pi])
                te, td = softmax_unit(bk.f32[0:nk, :], nk, 0, kind, barg, PT[pi][0:nk, :], tk)
                bk.free = [te] if td is None else [td]
                pend[(u, m_)] = (pi, te)

            def stageB(u, m_):
                kc0, nk, vslot, kind, barg = units[u]
                pi, te = pend.pop((u, m_))
                first, last_ = (u == 0), (u == len(units) - 1)
                PE.wait(te, acc[m_].free, acc[2 + m_].free)
                nc.tensor.matmul(acc[m_].f32[:, :], lhsT=Vb[0:nk, vslot, :], rhs=PT[pi][0:nk, :],
                                 start=first, stop=last_)
                tk = PE.mark(nc.tensor.matmul(acc[2 + m_].f32[:, :], lhsT=ones_b[0:nk, :], rhs=PT[pi][0:nk, :],
                                              start=first, stop=last_))
                pt_free[pi] = [tk]
                return tk

            stageA(*steps[0])
            stageA(*steps[1])
            tk_pv = None
            for i in range(2, len(steps) + 2):
                if i < len(steps):
                    stageA(*steps[i])
                tk_pv = stageB(*steps[i - 2])
                if B == 0 or i >= 12:
                    pull()
                    if i % 3 == 0:
                        pull_nxt()
                if i == 16 and deferred:
                    deferred.pop()()
            deferred.append(post_process(acc[0].f32[:, :], acc[1].f32[:, :], acc[2].f32[:, :], acc[3].f32[:, :],
                                         512, h, abT[:, 8 + h, q0:q0 + 512], tk_pv, acc))
        while pull():
            pass
        if nxt is not None:
            run_gen(nxt)
        carry.append(deferred.pop())
        buf_free[h % 2] = [PE.last] + [samp_prep[s_][0] for s_ in range(2)]
        bias_free = [last_bias_read[0]]
    if carry:
        carry.pop()()

    if "abTb" in dbg:
        dump_bf("abTb", lambda c: abT[:, 8 + c, :], nheads, NQ, abT_b_done, at=150 * K, ntot=8)

    if stop_after == "ATT":
        barrier()
        SP.wait(*[k.ds.total for k in kst], ds_pst[0].total, ds_pst[1].total)
        return nc, dbg_t

    barrier()
    n2T = mem.alloc("n2T", [128, 16, NQ], BF16, at=3 * K)
    xb = [mem.alloc(f"xbF{i}", [128, D], BF16, at=143 * K + i * 4 * K) for i in range(2)]
    xb_free[0] = []
    xb_free[1] = []
    junk = mem.alloc("junkF", [128, D], BF16, at=151 * K)
    gbc = mem.alloc("gbcF", [128, D], F32, at=155 * K)
    t_hres = ds_h.total
    t_g = ds_g.inc(nc.sync.dma_start(out=gbc[:], in_=gffn.ap().partition_broadcast(128)))
    for cb in range(1, 4):
        wo[cb].ready = wo[cb].ds.inc(ng.dma_start(out=wo[cb].t[:], in_=w_out_v[:, :, cb * 256:(cb + 1) * 256]))
    wgu = [Slot(mem.alloc(f"wgu{i}", [128, 16, 2, 256], BF16, at=165 * K + i * 16 * K), newdsem(f"d_wgu{i}")) for i in range(2)]
    wg_v = w_gate.ap().rearrange("(c p) n -> p c n", p=128)
    wu_v = w_up.ap().rearrange("(c p) n -> p c n", p=128)

    def wgu_issue(pair):
        sl = wgu[pair % 2]
        POOL.wait(sl.free)
        sl.ds.inc(ng.dma_start(out=sl.t[:, :, 0, :], in_=wg_v[:, :, pair * 256:(pair + 1) * 256]))
        sl.ready = sl.ds.inc(ng.dma_start(out=sl.t[:, :, 1, :], in_=wu_v[:, :, pair * 256:(pair + 1) * 256]))

    wgu_issue(0)
    wgu_issue(1)
    ringF = Ring(banks)
    h_tok = [[] for _ in range(9)]
    for cb in range(8):
        sl = wo[cb % 4]
        for tt in range(9):
            bk = ringF.next()
            PE.wait(sl.ready, bk.free, abT_a_done, abT_b_done)
            for c in range(16):
                tk = nc.tensor.matmul(bk.f32[:, 0:256], lhsT=abT[:, c, tt * 128:(tt + 1) * 128], rhs=sl.t[:, c, :],
                                      start=(c == 0), stop=(c == 15))
            tk = PE.mark(tk)
            DVE.wait(tk, t_hres)
            hv = hres[:, tt, cb * 256:(cb + 1) * 256]
            te = DVE(nv.tensor_tensor(out=hv, in0=bk.f32[:, 0:256], in1=hv, op=ALU.add))
            bk.free = [te]
            h_tok[tt].append(te)
        if cb + 4 < 8:
            POOL.wait(PE.last)
            sl.ready = sl.ds.inc(ng.dma_start(out=sl.t[:], in_=w_out_v[:, :, (cb + 4) * 256:(cb + 5) * 256]))
    n2T_done = []
    outproj_done = PE.last
    for tt in range(9):
        rel, back = rms_transpose(hres[:, tt, :], 128, t_g, n2T, tt * 128, h_tok[tt] + [outproj_done], ringF, [ACT, DVE])
        if pend_back:
            n2T_done += pend_back.pop()()
        pend_back.append(back)
    n2T_done += pend_back.pop()()

    if "n2T" in dbg:
        dump_bf("n2T", lambda c: n2T[:, c, :], 16, NQ, n2T_done, at=170 * K)

    if stop_after == "F":
        barrier()
        SP.wait(*[k.ds.total for k in kst], ds_pst[0].total, ds_pst[1].total)
        return nc, dbg_t

    NG = DFF // 512
    actT = [mem.alloc(f"actT{i}", [128, 4, NQ], BF16, at=111 * K + i * 9216) for i in range(2)]
    sgt = [mem.alloc(f"sgt{i}", [128, 512], F32, at=129 * K + i * 2048) for i in range(2)]
    wd = [Slot(mem.alloc(f"wd{i}", [128, 4, D], BF16, at=133 * K + i * 16 * K), newdsem(f"d_wd{i}")) for i in range(2)]
    wd_v = w_down.ap().rearrange("(g i p) n -> g p i n", p=128, i=4)
    f_norm_done = [PE.last, ACT.last, DVE.last]

    def wd_issue(g):
        sl = wd[g % 2]
        POOL.wait(sl.free, f_norm_done)
        sl.ready = sl.ds.inc(ng.dma_start(out=sl.t[:], in_=wd_v[g]))

    wd_issue(0)
    wd_issue(1)
    ringGU = Ring(banks[0:4])
    ringD = Ring(banks[4:8])
    act_ready = [[], []]
    act_free = [[], []]
    sg_free = [[], []]
    sg_i = [0]
    TB = ((0, 512), (512, 512), (1024, 128))

    def gate_up(g):
        toks = []
        for i in range(4):
            f = 4 * g + i
            pair, sub = f // 2, f % 2
            sl = wgu[pair % 2]
            tk_last = None
            for (c0, n) in TB:
                bA = ringGU.next()
                bB = ringGU.next()
                PE.wait(sl.ready, n2T_done, bA.free, bB.free)
                for c in range(16):
                    nc.tensor.matmul(bA.f32[:, 0:n], lhsT=sl.t[:, c, 0, sub * 128:(sub + 1) * 128],
                                     rhs=n2T[:, c, c0:c0 + n], start=(c == 0), stop=(c == 15))
                for c in range(16):
                    tk = nc.tensor.matmul(bB.f32[:, 0:n], lhsT=sl.t[:, c, 1, sub * 128:(sub + 1) * 128],
                                          rhs=n2T[:, c, c0:c0 + n], start=(c == 0), stop=(c == 15))
                tk_last = PE.mark(tk)
                si = sg_i[0] % 2
                sg_i[0] += 1
                ACT.wait(tk_last, sg_free[si])
                ts = ACT(ns.activation(out=sgt[si][:, 0:n], in_=bA.f32[:, 0:n], func=AF.Silu))
                bA.free = [ts]
                DVE.wait(ts, act_free[g % 2])
                tm = DVE(nv.tensor_tensor(out=actT[g % 2][:, i, c0:c0 + n], in0=bB.f32[:, 0:n], in1=sgt[si][:, 0:n],
                                          op=ALU.mult))
                bB.free = [tm]
                sg_free[si] = [tm]
                toks.append(tm)
            if sub == 1:
                sl.free = [tk_last]
                if pair + 2 < DFF // 256:
                    wgu_issue(pair + 2)
        act_ready[g % 2] = toks

    def down(g):
        sl = wd[g % 2]
        last = None
        for tt in range(9):
            for cb in range(4):
                bk = ringD.next()
                PE.wait(sl.ready, act_ready[g % 2], bk.free)
                for i in range(4):
                    tk = nc.tensor.matmul(bk.f32[:, :], lhsT=actT[g % 2][:, i, tt * 128:(tt + 1) * 128],
                                          rhs=sl.t[:, i, cb * 512:(cb + 1) * 512], start=(i == 0), stop=(i == 3))
                last = PE.mark(tk)
                DVE.wait(last)
                hv = hres[:, tt, cb * 512:(cb + 1) * 512]
                te = DVE(nv.tensor_tensor(out=hv, in0=bk.f32[:, :], in1=hv, op=ALU.add))
                bk.free = [te]
                h_tok[tt].append(te)
            if g == NG - 1:
                final_norm(tt)
        act_free[g % 2] = [last]
        sl.free = [last]
        if g + 2 < NG:
            wd_issue(g + 2)

    for tt in range(9):
        h_tok[tt] = []

    gbcH = mem.alloc("gbcH", [128, D], F32, at=197 * K)
    junkH = mem.alloc("junkH", [128, 1024], BF16, at=205 * K)
    ds_y = newdsem("d_y")
    ds_gH = newdsem("d_gH")
    t_gH = ds_gH.inc(nc.sync.dma_start(out=gbcH[:], in_=gfin.ap().partition_broadcast(128)))

    fin = {}

    def final_s1(tt):
        i = norm_i[0]
        norm_i[0] += 1
        c_ssq = C_SSQ + 3 * i
        ca, cb_, cc = cols[:, c_ssq:c_ssq + 1], cols[:, c_ssq + 1:c_ssq + 2], cols[:, c_ssq + 2:c_ssq + 3]
        src = hres[:, tt, :]
        ACT.wait(h_tok[tt])
        ACT(ns.activation(out=junkH[:, :], in_=src[:, 0:1024], func=AF.Square, accum_out=ca))
        t_b = ACT(ns.activation(out=junkH[:, :], in_=src[:, 1024:2048], func=AF.Square, accum_out=cb_))
        fin[tt] = [ca, cb_, cc, src, t_b]

    def final_s2(tt):
        ca, cb_, cc, src, t_b = fin[tt]
        DVE.wait(t_b)
        t_bias = DVE(nv.tensor_scalar(out=cb_, in0=cb_, scalar1=1.0 / D, scalar2=EPS, op0=ALU.mult, op1=ALU.add))
        ACT.wait(t_bias)
        fin[tt][4] = ACT(ns.activation(out=cc, in_=ca, func=AF.Sqrt, bias=cb_, scale=1.0 / D))

    def final_s3(tt):
        ca, cb_, cc, src, t_rt = fin[tt]
        DVE.wait(t_rt)
        DVE(nv.reciprocal(out=ca, in_=cc))
        DVE.wait(t_gH, h_tok[tt])
        t_y = DVE(nv.scalar_tensor_tensor(out=src, in0=src, scalar=ca, in1=gbcH[:, :], op0=ALU.mult, op1=ALU.mult))
        SP.wait(t_y)
        ds_y.inc(nc.sync.dma_start(out=y_o.ap()[tt * 128:(tt + 1) * 128, :], in_=src))

    def final_norm(tt):
        final_s1(tt)
        if tt >= 1:
            final_s2(tt - 1)
        if tt >= 2:
            final_s3(tt - 2)
        if tt == 8:
            final_s2(8)
            final_s3(7)
            final_s3(8)

    gate_up(0)
    for g in range(1, NG):
        gate_up(g)
        down(g - 1)
    down(NG - 1)

    SP.wait(*[k.ds.total for k in kst], ds_pst[0].total, ds_pst[1].total, ds_y.total)
    barrier()

    return nc, dbg_t


_PROG = {}


def _prep_inputs(inp):
    f = lambda a: np.ascontiguousarray(np.asarray(a, dtype=np.float32))
    xp, xsamp, meta = f(inp["x_prompt"]), f(inp["x_sample"]), f(inp["meta"])
    ckf, cvf, spf = f(inp["cache_k"])[0], f(inp["cache_v"])[0], f(inp["state_pool"])[0]
    shared = {
        "g_mix": f(inp["g_mix"])[0], "w_in": f(inp["w_in"])[0], "w_pool": f(inp["w_pool"])[0],
        "pool_scale": f(inp["pool_scale"])[0],
        "lam_in": np.stack([f(inp["lambda_q1"])[0], f(inp["lambda_k1"])[0],
                            f(inp["lambda_q2"])[0], f(inp["lambda_k2"])[0]]),
        "g_subln": f(inp["g_subln"])[0], "w_out": f(inp["w_out"])[0], "g_ffn": f(inp["g_ffn"])[0],
        "w_gate": f(inp["w_gate"])[0], "w_up": f(inp["w_up"])[0], "w_down": f(inp["w_down"])[0],
        "rel_bias": f(inp["rel_bias"]), "g_final": f(inp["g_final"]),
    }
    ohs = [_onehot_for_half(0), _onehot_for_half(1)]
    maps = []
    for c in range(8):
        b, half = c // 2, c % 2
        own = xp[b, half * 1024:(half + 1) * 1024]
        forn = xp[b, (1 - half) * 1024:(2 - half) * 1024]
        halo = meta if half == 0 else xp[b, 1008:1024]
        samp = xsamp[2 * c:2 * c + 2].reshape(128, D)
        m = dict(shared)
        m["xin"] = np.concatenate([own, forn, meta, halo, samp], axis=0)
        m["ck"] = ckf[2 * c:2 * c + 2]
        m["cv"] = cvf[2 * c:2 * c + 2]
        m["stp"] = spf[2 * c:2 * c + 2]
        m["onehot"] = ohs[half]
        maps.append(m)
    return maps


def kernel(**inp):
    if "nc" not in _PROG:
        _PROG["nc"] = build_program()[0]
    maps = _prep_inputs(inp)
    res = run_bass_kernel_spmd(_PROG["nc"], maps, core_ids=list(range(8))).results
    y_p = np.zeros((4, 2048, D), np.float32)
    y_s = np.zeros((16, 64, D), np.float32)
    k_p = np.zeros((1, 4, 2064, 8, 128), np.float32)
    v_p = np.zeros((1, 4, 2064, 8, 128), np.float32)
    p_p = np.zeros((1, 4, 15, 1024), np.float32)
    k_s = np.zeros((1, 16, 64, 8, 128), np.float32)
    v_s = np.zeros((1, 16, 64, 8, 128), np.float32)
    p_s = np.zeros((1, 16, 15, 1024), np.float32)
    for c in range(8):
        b, half = c // 2, c % 2
        r = res[c]
        y_p[b, half * 1024:(half + 1) * 1024] = r["y"][0:1024]
        y_s[2 * c:2 * c + 2] = r["y"][1024:1152].reshape(2, 64, D)
        for (dst_p, dst_s, src) in ((k_p, k_s, r["kout"]), (v_p, v_s, r["vout"])):
            dst_p[0, b, 16 + half * 1024:16 + (half + 1) * 1024] = src[0:1024]
            if half == 0:
                dst_p[0, b, 0:16] = src[1024:1040]
            dst_s[0, 2 * c:2 * c + 2] = src[1040:1168].reshape(2, 64, 8, 128)
        if half == 1:
            p_p[0, b] = r["pout"][1:16]
        p_s[0, 2 * c] = r["pout"][17:32]
        p_s[0, 2 * c + 1] = r["pout"][33:48]
    return (y_p, y_s, k_p, v_p, p_p, k_s, v_s, p_s)
```
